# Optimizing a Trainium2 kernel written in Bass

```python
import math
import jax
import jax.numpy as jnp
from jax import lax
import numpy as np

D_MODEL = 1024
BATCH = 2
SEQ = 8192
DEPTH = 4

HEAD_DIM = 64
GROUP_HEADS = 4
GROUP_WIDTH = GROUP_HEADS * HEAD_DIM
N_GROUPS = 4
MIX_WIDTH = N_GROUPS * GROUP_WIDTH
Q_BLOCK = 128

FORGET_BIAS_INIT = 2.0
DIFF_QK_DIM = HEAD_DIM // 2
DIFF_SUBLN_EPS = 1e-5
MOBA_BLOCK = 256
MOBA_TOPK = 3
MLA_Q_LORA = 256
MLA_KV_LORA = 128
MLA_NOPE = 64
MLA_ROPE = 32
MLA_V = HEAD_DIM
ROPE_THETA = 10000.0
REL_BUCKETS = 32
REL_MAX_DIST = 128
N_REL_HEADS = 2 * GROUP_HEADS
D_FF = 2752
CONV_WIDTH = 3
DEEPNORM_ALPHA = (2 * DEPTH) ** 0.25
DEEPNORM_BETA = (8 * DEPTH) ** -0.25
LN_EPS = 1e-5
RMS_EPS = 1e-6

FOX_COLS = 3 * GROUP_WIDTH + GROUP_HEADS
DIFF_COLS = 3 * GROUP_WIDTH
MOBA_COLS = 3 * GROUP_WIDTH
MLA_COLS = MLA_Q_LORA + MLA_KV_LORA + MLA_ROPE
N_IN = FOX_COLS + DIFF_COLS + MOBA_COLS + MLA_COLS
IN_SPLITS = (FOX_COLS, FOX_COLS + DIFF_COLS, FOX_COLS + DIFF_COLS + MOBA_COLS)

kernel_name = 'hybrid_fox_diff_moba_mla_convffn_deepnorm'


def _layernorm(t, g, b):
    t32 = t.astype(jnp.float32)
    mu = jnp.mean(t32, axis=-1, keepdims=True)
    var = jnp.mean(jnp.square(t32 - mu), axis=-1, keepdims=True)
    return ((t32 - mu) * lax.rsqrt(var + LN_EPS) * g + b).astype(t.dtype)


def _rmsnorm(t, g, eps=RMS_EPS):
    t32 = t.astype(jnp.float32)
    return (t32 * lax.rsqrt(jnp.mean(jnp.square(t32), axis=-1, keepdims=True) + eps) * g).astype(t.dtype)


def _heads(t, n):
    b, s, w = t.shape
    return t.reshape(b, s, n, w // n).transpose(0, 2, 1, 3)


def _merge(t):
    b, h, s, d = t.shape
    return t.transpose(0, 2, 1, 3).reshape(b, s, h * d)


def _rel_bucket(dist):
    max_exact = REL_BUCKETS // 2
    d_large = jnp.maximum(dist, max_exact).astype(jnp.float32)
    large = max_exact + (jnp.log(d_large / max_exact) / math.log(REL_MAX_DIST / max_exact)
                         * (REL_BUCKETS - max_exact)).astype(jnp.int32)
    large = jnp.minimum(large, REL_BUCKETS - 1)
    return jnp.where(dist < max_exact, dist, large)


def _rope_tables(s):
    inv = ROPE_THETA ** (-jnp.arange(0, MLA_ROPE // 2, dtype=jnp.float32) * 2.0 / MLA_ROPE)
    ang = jnp.arange(s, dtype=jnp.float32)[:, None] * inv[None, :]
    return jnp.cos(ang), jnp.sin(ang)


def _rope(t, cos, sin):
    t1, t2 = jnp.split(t, 2, axis=-1)
    cos = cos.astype(t.dtype)
    sin = sin.astype(t.dtype)
    return jnp.concatenate([t1 * cos - t2 * sin, t2 * cos + t1 * sin], axis=-1)


def _sweep_query_blocks(fn, *q_like):
    n = q_like[0].shape[2] // Q_BLOCK

    def split(t):
        t = t.reshape(t.shape[:2] + (n, Q_BLOCK) + t.shape[3:])
        return jnp.moveaxis(t, 2, 0)

    starts = jnp.arange(n, dtype=jnp.int32) * Q_BLOCK
    out = lax.map(lambda a: fn(a[0], *a[1:]), (starts,) + tuple(split(t) for t in q_like))
    out = jnp.moveaxis(out, 0, 2)
    return out.reshape(out.shape[:2] + (n * Q_BLOCK,) + out.shape[4:])


def fox_attention(q, k, v, log_f):
    s_len = k.shape[2]
    cum = jnp.cumsum(log_f, axis=-1)
    k_pos = jnp.arange(s_len)
    scale = q.shape[-1] ** -0.5

    def block(start, qb, cb):
        q_pos = start + jnp.arange(Q_BLOCK)
        s = jnp.einsum('bhqd,bhkd->bhqk', qb, k).astype(jnp.float32) * scale
        s = s + (cb[..., :, None] - cum[..., None, :])
        s = jnp.where(k_pos[None, :] <= q_pos[:, None], s, -jnp.inf)
        p = jax.nn.softmax(s, axis=-1).astype(v.dtype)
        return jnp.einsum('bhqk,bhkd->bhqd', p, v)

    return _sweep_query_blocks(block, q, cum)


def diff_attention(q, k, v, lam, bias_by_dist):
    s_len = k.shape[2]
    k_pos = jnp.arange(s_len)
    scale = q.shape[-1] ** -0.5

    def block(start, qb):
        q_pos = start + jnp.arange(Q_BLOCK)
        dist = q_pos[:, None] - k_pos[None, :]
        bias = bias_by_dist[:, jnp.clip(dist, 0, s_len - 1)]
        s = jnp.einsum('bhqnd,bhknd->bhnqk', qb, k).astype(jnp.float32) * scale + bias[None, :, None]
        s = jnp.where(dist >= 0, s, -jnp.inf)
        p = jax.nn.softmax(s, axis=-1)
        a = p[:, :, 0] - lam * p[:, :, 1]
        return jnp.einsum('bhqk,bhkd->bhqd', a.astype(v.dtype), v)

    return _sweep_query_blocks(block, q)


def moba_attention(q, k, v, bias_by_dist):
    b, h, s_len, dh = k.shape
    nb = max(-(-s_len // MOBA_BLOCK), MOBA_TOPK)
    pad = nb * MOBA_BLOCK - s_len
    kp = jnp.pad(k, ((0, 0), (0, 0), (0, pad), (0, 0)))
    vp = jnp.pad(v, ((0, 0), (0, 0), (0, pad), (0, 0)))
    kb = kp.reshape(b, h, nb, MOBA_BLOCK, dh)
    vb = vp.reshape(b, h, nb, MOBA_BLOCK, dh)
    k_mean = jnp.mean(kb.astype(jnp.float32), axis=3).astype(k.dtype)
    blk = jnp.arange(nb)
    within = jnp.arange(MOBA_BLOCK)
    b_ix = jnp.arange(b)[:, None, None, None]
    h_ix = jnp.arange(h)[None, :, None, None]
    scale = dh ** -0.5
    n_sel = MOBA_TOPK * MOBA_BLOCK

    def block(start, qb):
        q_pos = start + jnp.arange(Q_BLOCK)
        own = start // MOBA_BLOCK
        gate = jnp.einsum('bhqd,bhnd->bhqn', qb, k_mean).astype(jnp.float32)
        gate = jnp.where(blk < own, gate, -jnp.inf)
        _, sel = lax.top_k(gate, MOBA_TOPK)
        valid = jnp.arange(MOBA_TOPK) < own
        k_sel = kb[b_ix, h_ix, sel]
        v_sel = vb[b_ix, h_ix, sel]
        sel_dist = q_pos[:, None, None] - (sel[..., None] * MOBA_BLOCK + within)
        sel_bias = bias_by_dist[h_ix[..., None], jnp.clip(sel_dist, 0, s_len - 1)]
        s_sel = jnp.einsum('bhqd,bhqnkd->bhqnk', qb, k_sel).astype(jnp.float32) * scale + sel_bias
        s_sel = jnp.where(valid[:, None], s_sel, -jnp.inf).reshape(b, h, Q_BLOCK, n_sel)
        own_start = own * MOBA_BLOCK
        k_own = lax.dynamic_slice_in_dim(kp, own_start, MOBA_BLOCK, axis=2)
        v_own = lax.dynamic_slice_in_dim(vp, own_start, MOBA_BLOCK, axis=2)
        own_dist = q_pos[:, None] - (own_start + within)[None, :]
        own_bias = bias_by_dist[:, jnp.clip(own_dist, 0, s_len - 1)]
        s_own = jnp.einsum('bhqd,bhkd->bhqk', qb, k_own).astype(jnp.float32) * scale + own_bias[None]
        s_own = jnp.where(own_dist >= 0, s_own, -jnp.inf)
        p = jax.nn.softmax(jnp.concatenate([s_sel, s_own], axis=-1), axis=-1).astype(v.dtype)
        p_sel = p[..., :n_sel].reshape(b, h, Q_BLOCK, MOBA_TOPK, MOBA_BLOCK)
        return (jnp.einsum('bhqnk,bhqnkd->bhqd', p_sel, v_sel)
                + jnp.einsum('bhqk,bhkd->bhqd', p[..., n_sel:], v_own))

    return _sweep_query_blocks(block, q)


def causal_attention(q, k, v):
    s_len = k.shape[2]
    k_pos = jnp.arange(s_len)
    scale = q.shape[-1] ** -0.5

    def block(start, qb):
        q_pos = start + jnp.arange(Q_BLOCK)
        s = jnp.einsum('bhqd,bhkd->bhqk', qb, k).astype(jnp.float32) * scale
        s = jnp.where(k_pos[None, :] <= q_pos[:, None], s, -jnp.inf)
        p = jax.nn.softmax(s, axis=-1).astype(v.dtype)
        return jnp.einsum('bhqk,bhkd->bhqd', p, v)

    return _sweep_query_blocks(block, q)


def setup_inputs(seed: int = 0) -> dict:
    key = jax.random.key(seed)
    ks = jax.random.split(key, 20)
    f32 = jnp.float32
    L = DEPTH

    def nrm(k, shape, scale):
        return jax.random.normal(k, shape, f32) * scale

    return {
        'x': nrm(ks[0], (BATCH, SEQ, D_MODEL), 1.0),
        'w_in': nrm(ks[1], (L, D_MODEL, N_IN), D_MODEL ** -0.5),
        'b_forget': FORGET_BIAS_INIT + nrm(ks[2], (L, GROUP_HEADS), 0.1),
        'diff_lambda': nrm(ks[3], (L, 4, DIFF_QK_DIM), 0.1),
        'diff_subln': 1.0 + nrm(ks[4], (L, HEAD_DIM), 0.02),
        'mla_q_norm': 1.0 + nrm(ks[5], (L, MLA_Q_LORA), 0.02),
        'mla_kv_norm': 1.0 + nrm(ks[6], (L, MLA_KV_LORA), 0.02),
        'mla_w_uq': nrm(ks[7], (L, MLA_Q_LORA, GROUP_HEADS * (MLA_NOPE + MLA_ROPE)), MLA_Q_LORA ** -0.5),
        'mla_w_ukv': nrm(ks[8], (L, MLA_KV_LORA, GROUP_HEADS * (MLA_NOPE + MLA_V)), MLA_KV_LORA ** -0.5),
        'rel_bias': nrm(ks[9], (REL_BUCKETS, N_REL_HEADS), 0.2),
        'w_o': nrm(ks[10], (L, MIX_WIDTH, D_MODEL), MIX_WIDTH ** -0.5 * DEEPNORM_BETA),
        'ln1_g': 1.0 + nrm(ks[11], (L, D_MODEL), 0.02),
        'ln1_b': nrm(ks[12], (L, D_MODEL), 0.02),
        'w_up': nrm(ks[13], (L, D_MODEL, 2 * D_FF), D_MODEL ** -0.5),
        'conv_w': nrm(ks[14], (L, CONV_WIDTH, 2 * D_FF), CONV_WIDTH ** -0.5),
        'conv_b': nrm(ks[15], (L, 2 * D_FF), 0.02),
        'w_down': nrm(ks[16], (L, D_FF, D_MODEL), D_FF ** -0.5 * DEEPNORM_BETA),
        'ln2_g': 1.0 + nrm(ks[17], (L, D_MODEL), 0.02),
        'ln2_b': nrm(ks[18], (L, D_MODEL), 0.02),
    }


def reference(x, w_in, b_forget, diff_lambda, diff_subln, mla_q_norm, mla_kv_norm, mla_w_uq,
              mla_w_ukv, rel_bias, w_o, ln1_g, ln1_b, w_up, conv_w, conv_b, w_down, ln2_g, ln2_b):
    b, s_len, _ = x.shape
    G = GROUP_HEADS
    pos = jnp.arange(s_len)
    bias_by_dist = rel_bias[_rel_bucket(pos)].T.astype(jnp.float32)
    bias_diff, bias_moba = bias_by_dist[:G], bias_by_dist[G:]
    cos, sin = _rope_tables(s_len)

    for l in range(DEPTH):
        h = x @ w_in[l]
        fox_h, diff_h, moba_h, mla_h = jnp.split(h, list(IN_SPLITS), axis=-1)

        fq, fk, fv, ff = jnp.split(fox_h, [GROUP_WIDTH, 2 * GROUP_WIDTH, 3 * GROUP_WIDTH], axis=-1)
        log_f = jax.nn.log_sigmoid(ff.astype(jnp.float32) + b_forget[l].astype(jnp.float32)).transpose(0, 2, 1)
        fox_o = fox_attention(_heads(fq, G), _heads(fk, G), _heads(fv, G), log_f)

        lam_init = 0.8 - 0.6 * math.exp(-0.3 * l)
        lq1, lk1, lq2, lk2 = diff_lambda[l].astype(jnp.float32)
        lam = jnp.exp(jnp.sum(lq1 * lk1)) - jnp.exp(jnp.sum(lq2 * lk2)) + lam_init
        dq, dk, dv = jnp.split(diff_h, [GROUP_WIDTH, 2 * GROUP_WIDTH], axis=-1)
        dq = _heads(dq, G).reshape(b, G, s_len, 2, DIFF_QK_DIM)
        dk = _heads(dk, G).reshape(b, G, s_len, 2, DIFF_QK_DIM)
        diff_o = diff_attention(dq, dk, _heads(dv, G), lam, bias_diff)
        diff_o = _rmsnorm(diff_o, diff_subln[l], DIFF_SUBLN_EPS) * (1.0 - lam_init)

        mq, mk, mv = jnp.split(moba_h, [GROUP_WIDTH, 2 * GROUP_WIDTH], axis=-1)
        moba_o = moba_attention(_heads(mq, G), _heads(mk, G), _heads(mv, G), bias_moba)

        cq, ckv, kr = jnp.split(mla_h, [MLA_Q_LORA, MLA_Q_LORA + MLA_KV_LORA], axis=-1)
        lq = _heads(_rmsnorm(cq, mla_q_norm[l]) @ mla_w_uq[l], G)
        lkv = _heads(_rmsnorm(ckv, mla_kv_norm[l]) @ mla_w_ukv[l], G)
        q_nope, q_rope = jnp.split(lq, [MLA_NOPE], axis=-1)
        k_nope, lv = jnp.split(lkv, [MLA_NOPE], axis=-1)
        k_rope = jnp.broadcast_to(_rope(kr[:, None], cos, sin), q_rope.shape)
        mla_o = causal_attention(jnp.concatenate([q_nope, _rope(q_rope, cos, sin)], axis=-1),
                                 jnp.concatenate([k_nope, k_rope], axis=-1), lv)

        mix = jnp.concatenate([_merge(fox_o), _merge(diff_o), _merge(moba_o), _merge(mla_o)], axis=-1) @ w_o[l]
        x = _layernorm(DEEPNORM_ALPHA * x + mix, ln1_g[l], ln1_b[l])

        u = x @ w_up[l]
        u_pad = jnp.pad(u, ((0, 0), (CONV_WIDTH - 1, 0), (0, 0)))
        cw = conv_w[l]
        u = sum(cw[j] * u_pad[:, j:j + s_len] for j in range(CONV_WIDTH)) + conv_b[l]
        gate, val = jnp.split(u, 2, axis=-1)
        y = (jax.nn.silu(gate) * val) @ w_down[l]
        x = _layernorm(DEEPNORM_ALPHA * x + y, ln2_g[l], ln2_b[l])

    return x
```

```python
from contextlib import ExitStack
import math
import numpy as np
import ml_dtypes
import concourse.bass as bass
import concourse.mybir as mybir
from concourse.bass_utils import run_bass_kernel_spmd

F32 = mybir.dt.float32
BF16 = mybir.dt.bfloat16
AF = mybir.ActivationFunctionType
ALU = mybir.AluOpType
AX = mybir.AxisListType
NPBF = ml_dtypes.bfloat16

D_MODEL = 1024
BATCH = 2
SEQ = 8192
DEPTH = 4
D_FF = 2752
N_IN = 2724
ALPHA = (2 * DEPTH) ** 0.25
LN_EPS = 1e-5
RMS_EPS = 1e-6
NEG = -30000.0
NCORES = 8
TOK = 2048
SELF_SYNC = True


class EngW:
    def __init__(self, name, eng, sem, selfsync):
        self.name, self.eng, self.sem, self.count, self.seen, self.selfsync = name, eng, sem, 0, {}, selfsync


class DSem:
    def __init__(self, sem):
        self.sem, self.count = sem, 0


class Res:
    __slots__ = ("name", "lw", "rd")

    def __init__(self, name=""):
        self.name, self.lw, self.rd = name, None, {}


class KB:
    def __init__(self, nc):
        self.nc = nc
        self.es = ExitStack()
        self.n_sem = 0
        mk = lambda n, e, ss: EngW(n, e, self.sem(n), ss)
        self.pe = mk("pe", nc.tensor, False)
        self.act = mk("act", nc.scalar, SELF_SYNC)
        self.dve = mk("dve", nc.vector, SELF_SYNC)
        self.pool = mk("pool", nc.gpsimd, SELF_SYNC)
        self.sp = mk("sp", nc.sync, False)
        self.engs = [self.pe, self.act, self.dve, self.pool, self.sp]
        self.dsems = []
        self.uid = 0

    def sem(self, name):
        self.n_sem += 1
        return self.es.enter_context(self.nc.semaphore(f"s{self.n_sem}_{name}"))

    def dsem(self, name="d"):
        d = DSem(self.sem(name))
        self.dsems.append(d)
        return d

    def sb(self, name, shape, dtype):
        self.uid += 1
        return self.es.enter_context(self.nc.sbuf_tensor(f"{name}_{self.uid}", list(shape), dtype))

    def ps(self, name, shape, dtype=F32):
        self.uid += 1
        return self.es.enter_context(self.nc.psum_tensor(f"{name}_{self.uid}", list(shape), dtype))

    def _deps(self, e, reads, writes):
        need = {}

        def add(t):
            if t is None:
                return
            k, v = t
            if need.get(k, 0) < v:
                need[k] = v

        for r in reads:
            add(r.lw)
        for w in writes:
            add(w.lw)
            for k, v in w.rd.items():
                add((k, v))
        for k, v in need.items():
            if k is e and not e.selfsync:
                continue
            if e.seen.get(k, 0) >= v:
                continue
            e.eng.wait_ge(k.sem, v)
            e.seen[k] = v

    def _commit(self, tok, reads, writes):
        k, v = tok
        for w in writes:
            w.lw = tok
            w.rd = {}
        for r in reads:
            if r.rd.get(k, 0) < v:
                r.rd[k] = v

    def op(self, e, fn, reads=(), writes=()):
        self._deps(e, reads, writes)
        ins = fn()
        ins.then_inc(e.sem, 1)
        e.count += 1
        self._commit((e, e.count), reads, writes)
        return ins

    def dma(self, q, out, in_, reads=(), writes=(), ds=None, **kw):
        self._deps(q, reads, writes)
        ins = q.eng.dma_start(out=out, in_=in_, **kw)
        ins.then_inc(ds.sem, 16)
        ds.count += 16
        self._commit((ds, ds.count), reads, writes)
        return ins

    def finish(self):
        e = self.sp
        for d in self.dsems:
            if d.count and e.seen.get(d, 0) < d.count:
                e.eng.wait_ge(d.sem, d.count)
        for x in self.engs:
            if x is not e and x.count:
                e.eng.wait_ge(x.sem, x.count)
        self.es.close()


def R(*names):
    return [Res(n) for n in names]


NU = 44
BLK = 256


def stage_b(kb, io):
    nc = kb.nc
    pe, act, dve, pool, sp = kb.pe, kb.act, kb.dve, kb.pool, kb.sp
    OT, X, XO, XOT = io["OT"], io["x"], io["xo"], io["xoT"]

    wo = kb.sb("wo", [128, 8, 1024], BF16)
    wup = kb.sb("wup", [128, 8, 5504], BF16)
    wdn = kb.sb("wdn", [128, 22, 1024], BF16)
    cw = kb.sb("cw", [128, 3 * NU], F32)
    cb = kb.sb("cb", [128, NU], F32)
    lnp = kb.sb("lnp", [128, 4, 1024], F32)
    flag = kb.sb("flag", [128, 1], F32)
    ident = kb.sb("ident", [128, 128], F32)
    epsln = kb.sb("epsln", [128, 1], F32)
    r_wo, r_wup, r_wdn, r_c, r_id = R("wo", "wup", "wdn", "c", "id")
    dw = kb.dsem("w")
    for kc in range(8):
        kb.dma(pool, wo[:, kc, :], io["w_o"][kc * 128:(kc + 1) * 128, :], writes=[r_wo], ds=dw)
    for kc in range(8):
        for h in range(4):
            kb.dma(pool, wup[:, kc, h * 1376:(h + 1) * 1376], io["w_up"][kc * 128:(kc + 1) * 128, h * 1376:(h + 1) * 1376],
                   writes=[r_wup], ds=dw)
    for i in range(22):
        n = 128 if i < 21 else 64
        kb.dma(pool, wdn[0:n, i, :], io["w_down"][i * 128:i * 128 + n, :], writes=[r_wdn], ds=dw)
    dc = kb.dsem("c")
    kb.dma(sp, cw[:], io["cw"][:, :], writes=[r_c], ds=dc)
    kb.dma(sp, cb[:], io["cb"][:, :], writes=[r_c], ds=dc)
    kb.dma(sp, flag[:], io["flag"][:, :], writes=[r_c], ds=dc)
    kb.dma(sp, ident[:], io["ident"][:, :], writes=[r_id], ds=dc)
    for i, nm in enumerate(["ln1_g", "ln1_b", "ln2_g", "ln2_b"]):
        kb.dma(sp, lnp[:, i, :], io[nm][:, :], writes=[r_c], ds=dc)
    kb.op(dve, lambda: nc.vector.memset(epsln[:], LN_EPS), writes=[r_c])

    NB = 2
    otc = [kb.sb("otc", [128, 8, 128], BF16) for _ in range(NB)]
    r_otc = R(*["otc"] * NB)
    d_otc = [kb.dsem("otc") for _ in range(NB)]
    xt = [kb.sb("xt", [128, 1024], F32) for _ in range(NB)]
    r_xt = R(*["xt"] * NB)
    d_xt = [kb.dsem("xt") for _ in range(NB)]
    stats = kb.sb("stats", [128, 2, 6], F32)
    mv = kb.sb("mv", [128, 2], F32)
    rstd = kb.sb("rstd", [128, 1], F32)
    r_st = Res("st")
    x1 = kb.sb("x1", [128, 2, 1024], F32)
    r_x1 = R("x1a", "x1b")
    x1T = kb.sb("x1T", [128, 8, BLK], BF16)
    r_x1T = R("x1Ta", "x1Tb")
    xhT = kb.sb("xhT", [128, 8, 2], BF16)
    r_xhT = Res("xhT")
    hT = [kb.sb("hT", [128, BLK], BF16) for _ in range(2)]
    r_hT = R("hT0", "hT1")
    carry = kb.sb("carry", [128, NU, 2], F32)
    r_carry = R(*[f"cy{i}" for i in range(NU)])
    usb = [kb.sb("usb", [128, BLK + 2], F32) for _ in range(4)]
    r_usb = R(*["usb"] * 4)
    uc = [kb.sb("uc", [128, BLK], F32) for _ in range(4)]
    r_uc = R(*["uc"] * 4)
    sg = [kb.sb("sg", [128, BLK], F32) for _ in range(2)]
    r_sg = R(*["sg"] * 2)
    x2 = [kb.sb("x2", [128, 1024], F32)] * 2
    r_x2 = [Res("x2")] * 2
    d_x2 = [kb.dsem("x2")] * 2
    x2T = [kb.sb("x2T", [128, 8, 128], BF16) for _ in range(2)]
    r_x2T = R(*["x2T"] * 2)
    d_x2T = [kb.dsem("x2T") for _ in range(2)]

    pA = [kb.ps("pA", [128, 512]) for _ in range(4)]
    r_pA = R("pA0", "pA1", "pA2", "pA3")
    pu = [kb.ps("pu", [128, 512]) for _ in range(4)]
    r_pu = R(*["pu"] * 4)

    cnt = {"ld": 0, "u": 0, "sg": 0, "o": 0, "h": 0}

    def layernorm(src, r_src, gi):
        for h in range(2):
            kb.op(dve, lambda h=h: nc.vector.bn_stats(out=stats[:, h, :], in_=src[:, h * 512:(h + 1) * 512]),
                  reads=[r_src], writes=[r_st])
        kb.op(dve, lambda: nc.vector.bn_aggr(out=mv[:], in_=stats[:].rearrange("p a b -> p (a b)")), reads=[r_st], writes=[r_st])
        kb.op(act, lambda: nc.scalar.activation(out=rstd[:], in_=mv[:, 1:2], func=AF.Sqrt, bias=epsln[:, 0:1], scale=1.0),
              reads=[r_st, r_c], writes=[r_st])
        kb.op(dve, lambda: nc.vector.reciprocal(out=rstd[:], in_=rstd[:]), reads=[r_st], writes=[r_st])
        kb.op(dve, lambda: nc.vector.tensor_scalar(out=src, in0=src, scalar1=mv[:, 0:1], scalar2=rstd[:, 0:1],
                                                    op0=ALU.subtract, op1=ALU.mult), reads=[r_src, r_st], writes=[r_src])
        kb.op(pool, lambda: nc.gpsimd.tensor_tensor(out=src, in0=src, in1=lnp[:, gi, :], op=ALU.mult),
              reads=[r_src, r_c], writes=[r_src])
        kb.op(dve, lambda: nc.vector.tensor_tensor(out=src, in0=src, in1=lnp[:, gi + 1, :], op=ALU.add),
              reads=[r_src, r_c], writes=[r_src])

    def transpose_to(srcf32, r_src, dstT, r_dst, ncols, col0=0):
        for half in range(2):
            i = cnt["u"] % 4
            cnt["u"] += 1
            for q in range(4):
                kc = half * 4 + q
                kb.op(pe, lambda kc=kc, q=q, i=i: nc.tensor.transpose(pu[i][:, q * 128:(q + 1) * 128], srcf32[:, kc * 128:(kc + 1) * 128], ident[:]),
                      reads=[r_src, r_id], writes=[r_pu[i]])
            kb.op(act, lambda half=half, i=i: nc.scalar.copy(
                out=dstT[:, half * 4:(half + 1) * 4, :],
                in_=pu[i][:].rearrange("p (q t) -> p q t", q=4)[:, :, col0:col0 + ncols]),
                reads=[r_pu[i]], writes=[r_dst])

    def mix_ln1(tcol, trow, dst, r_dst):
        i = cnt["ld"] % NB
        cnt["ld"] += 1
        kb.dma(sp, otc[i][:], OT[:, tcol:tcol + 128].rearrange("(kc p) t -> p kc t", p=128), writes=[r_otc[i]], ds=d_otc[i])
        kb.dma(sp, xt[i][:], X[trow:trow + 128, :], writes=[r_xt[i]], ds=d_xt[i])
        for h in range(2):
            for kc in range(8):
                kb.op(pe, lambda h=h, kc=kc: nc.tensor.matmul(pA[h][:], lhsT=otc[i][:, kc, :], rhs=wo[:, kc, h * 512:(h + 1) * 512],
                                                               start=(kc == 0), stop=(kc == 7)),
                      reads=[r_otc[i], r_wo], writes=[r_pA[h]])
        for h in range(2):
            kb.op(dve, lambda h=h: nc.vector.scalar_tensor_tensor(out=dst[:, h * 512:(h + 1) * 512], in0=xt[i][:, h * 512:(h + 1) * 512],
                                                                    scalar=ALPHA, in1=pA[h][:], op0=ALU.mult, op1=ALU.add),
                  reads=[r_xt[i], r_pA[h]], writes=[r_dst])
        layernorm(dst, r_dst, 0)

    def down(i, hb, n):
        nf = 128 if i < 21 else 64
        for t in range(n // 128):
            for h in range(2):
                kb.op(pe, lambda h=h, t=t: nc.tensor.matmul(pA[2 * t + h][:], lhsT=hT[hb][0:nf, t * 128:(t + 1) * 128],
                                                             rhs=wdn[0:nf, i, h * 512:(h + 1) * 512], start=(i == 0), stop=(i == 21)),
                      reads=[r_hT[hb], r_wdn], writes=[r_pA[2 * t + h]])

    def ffn(rhs_of_kc, r_rhs, n, halo):
        pend = None
        for i in range(22):
            nf = 128 if i < 21 else 64
            bufs = []
            for which in range(2):
                u = i + 22 * which
                f0 = 128 * i + D_FF * which
                b = cnt["u"] % 4
                cnt["u"] += 1
                bufs.append(b)
                for kc in range(8):
                    kb.op(pe, lambda kc=kc, f0=f0, b=b: nc.tensor.matmul(pu[b][0:nf, 0:n], lhsT=wup[:, kc, f0:f0 + nf], rhs=rhs_of_kc(kc),
                                                                          start=(kc == 0), stop=(kc == 7)),
                          reads=[*r_rhs, r_wup], writes=[r_pu[b]])
                if halo:
                    kb.op(dve, lambda u=u, b=b: nc.vector.tensor_scalar(out=carry[0:nf, u, :], in0=pu[b][0:nf, 0:2], scalar1=flag[0:nf, 0:1],
                                                                          scalar2=None, op0=ALU.mult),
                          reads=[r_pu[b], r_c], writes=[r_carry[u]])
                    continue
                kb.op(pool, lambda u=u, b=b: nc.gpsimd.tensor_copy(out=usb[b][0:nf, 0:2], in_=carry[0:nf, u, :]),
                      reads=[r_carry[u]], writes=[r_usb[b]])
                kb.op(act, lambda b=b: nc.scalar.copy(out=usb[b][0:nf, 2:2 + n], in_=pu[b][0:nf, 0:n]),
                      reads=[r_pu[b]], writes=[r_usb[b]])
                kb.op(pool, lambda u=u, b=b: nc.gpsimd.tensor_copy(out=carry[0:nf, u, :], in_=usb[b][0:nf, n:n + 2]),
                      reads=[r_usb[b]], writes=[r_carry[u]])
                kb.op(act, lambda u=u, b=b: nc.scalar.activation(out=uc[b][0:nf, 0:n], in_=usb[b][0:nf, 2:2 + n], func=AF.Identity,
                                                                  bias=cb[0:nf, u:u + 1], scale=cw[0:nf, 2 * NU + u:2 * NU + u + 1]),
                      reads=[r_usb[b], r_c], writes=[r_uc[b]])
                kb.op(dve, lambda u=u, b=b: nc.vector.scalar_tensor_tensor(out=uc[b][0:nf, 0:n], in0=usb[b][0:nf, 1:1 + n],
                                                                            scalar=cw[0:nf, NU + u:NU + u + 1], in1=uc[b][0:nf, 0:n],
                                                                            op0=ALU.mult, op1=ALU.add),
                      reads=[r_usb[b], r_uc[b], r_c], writes=[r_uc[b]])
                kb.op(dve, lambda u=u, b=b: nc.vector.scalar_tensor_tensor(out=uc[b][0:nf, 0:n], in0=usb[b][0:nf, 0:n],
                                                                             scalar=cw[0:nf, u:u + 1], in1=uc[b][0:nf, 0:n],
                                                                             op0=ALU.mult, op1=ALU.add),
                      reads=[r_usb[b], r_uc[b], r_c], writes=[r_uc[b]])
            if halo:
                continue
            bg, bv = bufs
            s = cnt["sg"] % 2
            cnt["sg"] += 1
            hb = cnt["h"] % 2
            cnt["h"] += 1
            kb.op(act, lambda bg=bg, s=s: nc.scalar.activation(out=sg[s][0:nf, 0:n], in_=uc[bg][0:nf, 0:n], func=AF.Silu),
                  reads=[r_uc[bg]], writes=[r_sg[s]])
            kb.op(dve, lambda hb=hb, bv=bv, s=s: nc.vector.tensor_tensor(out=hT[hb][0:nf, 0:n], in0=sg[s][0:nf, 0:n], in1=uc[bv][0:nf, 0:n], op=ALU.mult),
                  reads=[r_sg[s], r_uc[bv]], writes=[r_hT[hb]])
            if pend is not None:
                down(*pend, n)
            pend = (i, hb)
        if pend is not None:
            down(*pend, n)

    mix_ln1(0, 0, x1[:, 0, :], r_x1[0])
    transpose_to(x1[:, 0, :], r_x1[0], xhT, r_xhT, 2, col0=126)
    ffn(lambda kc: xhT[:, kc, :], [r_xhT], 2, True)

    for blk in range(TOK // BLK):
        for t in range(2):
            tok0 = blk * BLK + t * 128
            mix_ln1(128 + tok0, 128 + tok0, x1[:, t, :], r_x1[t])
            transpose_to(x1[:, t, :], r_x1[t], x1T[:, :, t * 128:(t + 1) * 128], r_x1T[t], 128)
        ffn(lambda kc: x1T[:, kc, :], r_x1T, BLK, False)
        for t in range(2):
            tok0 = blk * BLK + t * 128
            o = cnt["o"] % 2
            cnt["o"] += 1
            for h in range(2):
                kb.op(dve, lambda h=h, t=t, o=o: nc.vector.scalar_tensor_tensor(out=x2[o][:, h * 512:(h + 1) * 512], in0=x1[:, t, h * 512:(h + 1) * 512],
                                                                                 scalar=ALPHA, in1=pA[2 * t + h][:], op0=ALU.mult, op1=ALU.add),
                      reads=[r_x1[t], r_pA[2 * t + h]], writes=[r_x2[o]])
            layernorm(x2[o][:], r_x2[o], 2)
            kb.dma(sp, XO[tok0:tok0 + 128, :], x2[o][:], reads=[r_x2[o]], ds=d_x2[o])
            transpose_to(x2[o], r_x2[o], x2T[o], r_x2T[o], 128)
            kb.dma(sp, XOT[:, tok0:tok0 + 128].rearrange("(kc p) t -> p kc t", p=128), x2T[o][:], reads=[r_x2T[o]], ds=d_x2T[o])


def build_b():
    nc = bass.Bass("TRN2", target_bir_lowering=False)
    io = {}

    def din(name, shape, dt):
        io[name] = nc.dram_tensor(name, list(shape), dt, kind="ExternalInput").ap()

    din("OT", [1024, 128 + TOK], BF16)
    din("x", [128 + TOK, 1024], F32)
    din("flag", [128, 1], F32)
    din("w_o", [1024, 1024], F32)
    din("w_up", [1024, 5504], F32)
    din("w_down", [D_FF, 1024], F32)
    din("cw", [128, 3 * NU], F32)
    din("cb", [128, NU], F32)
    din("ident", [128, 128], F32)
    for nm in ["ln1_g", "ln1_b", "ln2_g", "ln2_b"]:
        din(nm, [128, 1024], F32)
    io["xo"] = nc.dram_tensor("xo", [TOK, 1024], F32, kind="ExternalOutput").ap()
    io["xoT"] = nc.dram_tensor("xoT", [1024, TOK], BF16, kind="ExternalOutput").ap()
    kb = KB(nc)
    stage_b(kb, io)
    kb.finish()
    return nc


def ffn_unit_layout(v):
    lead = v.shape[:-1]
    out = np.zeros((128,) + lead + (NU,), np.float32)
    for which in range(2):
        for i in range(22):
            nf = 128 if i < 21 else 64
            f0 = 128 * i + D_FF * which
            out[:nf, ..., i + 22 * which] = np.moveaxis(v[..., f0:f0 + nf], -1, 0)
    return out


def b_weight_maps(l, w_o, w_up, w_down, conv_w, conv_b, ln1_g, ln1_b, ln2_g, ln2_b):
    bc = lambda v: np.ascontiguousarray(np.broadcast_to(v[None, :], (128, 1024)))
    return {
        "w_o": np.ascontiguousarray(w_o[l]), "w_up": np.ascontiguousarray(w_up[l]), "w_down": np.ascontiguousarray(w_down[l]),
        "cw": np.ascontiguousarray(ffn_unit_layout(conv_w[l]).reshape(128, 3 * NU)),
        "cb": np.ascontiguousarray(ffn_unit_layout(conv_b[l])),
        "ident": np.eye(128, dtype=np.float32),
        "ln1_g": bc(ln1_g[l]), "ln1_b": bc(ln1_b[l]), "ln2_g": bc(ln2_g[l]), "ln2_b": bc(ln2_b[l]),
    }


HEADS = [
    ("fox", 67, [0], 67, 0, 0, 0),
    ("diff", 64, [0, 32], 32, 1, -1, 1),
    ("moba", 96, [0], 96, 2, -1, 2),
    ("mla", 96, [0], 96, 0, 0, 3),
]
NG = SEQ // 512
NT = SEQ // 128


def stage_a2(kb, io):
    nc = kb.nc
    pe, act, dve, pool, sp = kb.pe, kb.act, kb.dve, kb.pool, kb.sp
    OTc = io["OTc"]

    TBt = kb.sb("TB", [128, 3, 640], BF16)
    KBt = kb.sb("KB", [128, 8, 64], F32)
    identb = kb.sb("identb", [128, 128], BF16)
    onesf = kb.sb("onesf", [128, 64], F32)
    dlam = kb.sb("dlam", [1, 128], F32)
    dsub = kb.sb("dsub", [64, 1], F32)
    neglam = kb.sb("neglam", [64, 1], F32)
    lsc = kb.sb("lsc", [1, 8], F32)
    epsd = kb.sb("epsd", [64, 1], F32)
    r_c = Res("consts")
    dc = kb.dsem("a2c")
    kb.dma(pool, TBt[:], io["TB"][:, :, :], writes=[r_c], ds=dc)
    kb.dma(sp, KBt[:], io["KB"][:, :, :], writes=[r_c], ds=dc)
    kb.dma(sp, identb[:], io["identb"][:, :], writes=[r_c], ds=dc)
    kb.dma(sp, dlam[:], io["dlam"][:, :], writes=[r_c], ds=dc)
    kb.dma(sp, dsub[:], io["dsub"][:, :], writes=[r_c], ds=dc)
    lamc = kb.sb("lamc", [64, 2], F32)
    kb.dma(sp, lamc[:], io["lamc"][:, :], writes=[r_c], ds=dc)
    kb.op(dve, lambda: nc.vector.memset(onesf[:], 1.0), writes=[r_c])
    kb.op(dve, lambda: nc.vector.memset(epsd[:], 1e-5), writes=[r_c])
    r_l = Res("lam")
    kb.op(dve, lambda: nc.vector.tensor_tensor(out=dlam[:, 0:32], in0=dlam[:, 0:32], in1=dlam[:, 32:64], op=ALU.mult), reads=[r_c], writes=[r_l])
    kb.op(dve, lambda: nc.vector.tensor_tensor(out=dlam[:, 64:96], in0=dlam[:, 64:96], in1=dlam[:, 96:128], op=ALU.mult), reads=[r_l], writes=[r_l])
    kb.op(dve, lambda: nc.vector.reduce_sum(out=lsc[:, 0:1], in_=dlam[:, 0:32], axis=AX.X), reads=[r_l], writes=[r_l])
    kb.op(dve, lambda: nc.vector.reduce_sum(out=lsc[:, 1:2], in_=dlam[:, 64:96], axis=AX.X), reads=[r_l], writes=[r_l])
    kb.op(act, lambda: nc.scalar.activation(out=lsc[:, 2:4], in_=lsc[:, 0:2], func=AF.Exp), reads=[r_l], writes=[r_l])
    kb.op(dve, lambda: nc.vector.tensor_tensor(out=lsc[:, 4:5], in0=lsc[:, 3:4], in1=lsc[:, 2:3], op=ALU.subtract), reads=[r_l], writes=[r_l])
    kb.op(dve, lambda: nc.vector.tensor_scalar(out=lsc[:, 5:6], in0=lsc[:, 4:5], scalar1=lamc[0:1, 0:1], scalar2=None, op0=ALU.add), reads=[r_l, r_c], writes=[r_l])
    pbc = kb.ps("pbc", [128, 512])
    r_pbc = Res("pbc")
    kb.op(pe, lambda: nc.tensor.matmul(pbc[0:64, 0:1], lhsT=onesf[0:1, 0:64], rhs=lsc[0:1, 5:6], start=True, stop=True), reads=[r_l, r_c], writes=[r_pbc])
    kb.op(dve, lambda: nc.vector.tensor_copy(out=neglam[:], in_=pbc[0:64, 0:1]), reads=[r_pbc], writes=[r_l])
    kb.op(dve, lambda: nc.vector.tensor_scalar(out=dsub[:], in0=dsub[:], scalar1=lamc[:, 1:2], scalar2=None, op0=ALU.mult), reads=[r_c], writes=[r_l])

    KA = [kb.sb("KA", [128, SEQ], BF16) for _ in range(2)]
    VA = [kb.sb("VA", [128, NT, 65], BF16) for _ in range(2)]
    r_KA, r_VA = R("KA0", "KA1"), R("VA0", "VA1")
    d_KA, d_VA = [kb.dsem("KA") for _ in range(2)], [kb.dsem("VA") for _ in range(2)]
    NQ = 3
    QC = [kb.sb("QC", [128, 512], BF16) for _ in range(NQ)]
    r_QC = R(*["QC"] * NQ)
    d_QC = [kb.dsem("QC") for _ in range(NQ)]
    NS = 3
    Sp = [kb.ps("S", [128, 512]) for _ in range(NS)]
    r_S = R(*["S"] * NS)
    Pt = [kb.sb("Pt", [128, 512], BF16) for _ in range(NS)]
    r_P = R(*["P"] * NS)
    NO = 4
    Op = [kb.ps("O", [128, 512]) for _ in range(NO)]
    r_O = R(*["O"] * NO)
    osb = [kb.sb("osb", [65, 512], F32) for _ in range(2)]
    r_osb = R("osb0", "osb1")
    onrm = [kb.sb("on", [64, 512], F32) for _ in range(2)]
    r_on = R("on0", "on1")
    ostage = [kb.sb("ost", [64, 512], BF16) for _ in range(2)]
    r_ost = R("ost0", "ost1")
    d_ost = [kb.dsem("ost") for _ in range(2)]
    cnt = {"q": 0, "s": 0, "o": 0, "f": 0, "st": 0}

    def load_head(hi):
        name, rows = HEADS[hi][0], HEADS[hi][1]
        b = hi % 2
        for c in range(4):
            kb.dma(sp, KA[b][0:rows, c * 2048:(c + 1) * 2048], io["KA_" + name][:, c * 2048:(c + 1) * 2048], writes=[r_KA[b]], ds=d_KA[b])
        for c in range(4):
            kb.dma(sp, VA[b][:, c * 16:(c + 1) * 16, :], io["V_" + name][:, c * 16:(c + 1) * 16, :], writes=[r_VA[b]], ds=d_VA[b])

    load_head(0)
    deferred = []

    def tick():
        for d in deferred:
            d[0] -= 1
        while deferred and deferred[0][0] <= 0:
            deferred.pop(0)[1]()

    for hi, (name, rows, maps, crow, tbi, band_from, kbi) in enumerate(HEADS):
        hb = hi % 2
        if hi + 1 < len(HEADS):
            load_head(hi + 1)
        row0 = 64 * hi
        for G in range(NG):
            qb = cnt["q"] % NQ
            cnt["q"] += 1
            kb.dma(sp, QC[qb][0:rows, :], io["QA_" + name][:, G * 512:(G + 1) * 512], writes=[r_QC[qb]], ds=d_QC[qb])
            obanks = []
            steps = []
            for mi, p0 in enumerate(maps):
                ob = cnt["o"] % NO
                cnt["o"] += 1
                obanks.append(ob)
                for j in range(4 * G + 4):
                    steps.append((mi, p0, ob, j))
            LOOK = 2
            sbuf_of = {}

            def emit_qk(si):
                mi, p0, ob, j = steps[si]
                s = cnt["s"] % NS
                cnt["s"] += 1
                sbuf_of[si] = s
                c0 = 128 * max(0, j - 4 * G)
                n = 512 - c0
                band = j >= 4 * G + band_from
                kb.op(pe, lambda: nc.tensor.matmul(Sp[s][:, 0:n], lhsT=KA[hb][p0:p0 + crow, j * 128:(j + 1) * 128], rhs=QC[qb][p0:p0 + crow, c0:512],
                                                   start=True, stop=not band),
                      reads=[r_KA[hb], r_QC[qb]], writes=[r_S[s]])
                if band:
                    w0 = 128 if j == 4 * G - 1 else 0
                    kb.op(pe, lambda: nc.tensor.matmul(Sp[s][:, 0:n], lhsT=identb[:], rhs=TBt[:, tbi, w0:w0 + n], start=False, stop=True),
                          reads=[r_c], writes=[r_S[s]])

            def emit_rest(si):
                mi, p0, ob, j = steps[si]
                s = sbuf_of.pop(si)
                c0 = 128 * max(0, j - 4 * G)
                n = 512 - c0
                band = j >= 4 * G + band_from
                bias = KBt[:, 2 * kbi + (1 if band else 0), j:j + 1]
                kb.op(act, lambda: nc.scalar.activation(out=Pt[s][:, 0:n], in_=Sp[s][:, 0:n], func=AF.Exp, bias=bias, scale=1.0),
                      reads=[r_S[s], r_c], writes=[r_P[s]])
                kb.op(pe, lambda: nc.tensor.matmul(Op[ob][0:65, c0:512], lhsT=VA[hb][:, j, :], rhs=Pt[s][:, 0:n],
                                                   start=(j == 0), stop=(j == 4 * G + 3), skip_group_check=True),
                      reads=[r_VA[hb], r_P[s]], writes=[r_O[ob]])

            for si in range(min(LOOK, len(steps))):
                emit_qk(si)
            for si in range(len(steps)):
                if si + LOOK < len(steps):
                    emit_qk(si + LOOK)
                emit_rest(si)
                tick()

            def fin1(obanks=obanks, G=G, row0=row0, nm=len(maps)):
                ons = []
                for mi, ob in enumerate(obanks):
                    f = cnt["f"] % 2
                    cnt["f"] += 1
                    ons.append(f)
                    kb.op(dve, lambda: nc.vector.tensor_copy(out=osb[f][:], in_=Op[ob][0:65, :]), reads=[r_O[ob]], writes=[r_osb[f]])
                    kb.op(dve, lambda: nc.vector.reciprocal(out=osb[f][64:65, :], in_=osb[f][64:65, :]), reads=[r_osb[f]], writes=[r_osb[f]])

                def fin2():
                    for f in ons:
                        kb.op(pe, lambda: nc.tensor.matmul(pbc[0:64, :], lhsT=onesf[64:65, 0:64], rhs=osb[f][64:65, :], start=True, stop=True),
                              reads=[r_osb[f], r_c], writes=[r_pbc])
                        kb.op(dve, lambda: nc.vector.tensor_tensor(out=onrm[f][:], in0=osb[f][0:64, :], in1=pbc[0:64, :], op=ALU.mult),
                              reads=[r_osb[f], r_pbc], writes=[r_on[f]])
                    st = cnt["st"] % 2
                    cnt["st"] += 1
                    if nm == 1:
                        f = ons[0]
                        kb.op(dve, lambda: nc.vector.tensor_copy(out=ostage[st][:], in_=onrm[f][:]), reads=[r_on[f]], writes=[r_ost[st]])
                    else:
                        f0, f1 = ons
                        kb.op(dve, lambda: nc.vector.scalar_tensor_tensor(out=onrm[f0][:], in0=onrm[f1][:], scalar=neglam[:, 0:1], in1=onrm[f0][:],
                                                                          op0=ALU.mult, op1=ALU.add),
                              reads=[r_on[f0], r_on[f1], r_l], writes=[r_on[f0]])
                        kb.op(pool, lambda: nc.gpsimd.tensor_tensor(out=onrm[f1][:], in0=onrm[f0][:], in1=onrm[f0][:], op=ALU.mult),
                              reads=[r_on[f0]], writes=[r_on[f1]])
                        kb.op(pe, lambda: nc.tensor.matmul(pbc[0:64, :], lhsT=onesf[0:64, 0:64], rhs=onrm[f1][:], start=True, stop=True),
                              reads=[r_on[f1], r_c], writes=[r_pbc])
                        kb.op(act, lambda: nc.scalar.activation(out=onrm[f1][:], in_=pbc[0:64, :], func=AF.Sqrt, bias=epsd[:, 0:1], scale=1.0 / 64),
                              reads=[r_pbc, r_c], writes=[r_on[f1]])
                        kb.op(dve, lambda: nc.vector.reciprocal(out=onrm[f1][:], in_=onrm[f1][:]), reads=[r_on[f1]], writes=[r_on[f1]])
                        kb.op(dve, lambda: nc.vector.tensor_tensor(out=onrm[f0][:], in0=onrm[f0][:], in1=onrm[f1][:], op=ALU.mult),
                              reads=[r_on[f0], r_on[f1]], writes=[r_on[f0]])
                        kb.op(dve, lambda: nc.vector.tensor_scalar(out=ostage[st][:], in0=onrm[f0][:], scalar1=dsub[:, 0:1], scalar2=None, op0=ALU.mult),
                              reads=[r_on[f0], r_l], writes=[r_ost[st]])
                    kb.dma(sp, OTc[row0:row0 + 64, G * 512:(G + 1) * 512], ostage[st][:], reads=[r_ost[st]], ds=d_ost[st])

                deferred.append([6, fin2])

            deferred.append([3, fin1])
    while deferred:
        tick()


def build_a2():
    nc = bass.Bass("TRN2", target_bir_lowering=False)
    io = {}

    def din(name, shape, dt):
        io[name] = nc.dram_tensor(name, list(shape), dt, kind="ExternalInput").ap()

    for name, rows, *_ in HEADS:
        din("QA_" + name, [rows, SEQ], BF16)
        din("KA_" + name, [rows, SEQ], BF16)
        din("V_" + name, [128, NT, 65], BF16)
    din("KB", [128, 8, 64], F32)
    din("TB", [128, 3, 640], F32)
    din("identb", [128, 128], BF16)
    din("dlam", [1, 128], F32)
    din("dsub", [64, 1], F32)
    din("lamc", [64, 2], F32)
    io["OTc"] = nc.dram_tensor("OTc", [256, SEQ], BF16, kind="ExternalOutput").ap()
    kb = KB(nc)
    stage_a2(kb, io)
    kb.finish()
    return nc


def rel_bucket_np(dist):
    max_exact = 16
    d_large = np.maximum(dist, max_exact).astype(np.float32)
    large = max_exact + (np.log(d_large / max_exact) / math.log(128 / max_exact) * (32 - max_exact)).astype(np.int32)
    large = np.minimum(large, 31)
    return np.where(dist < max_exact, dist, large)


def band_tables(rel_bias, j):
    m = np.arange(640)[None, :]
    k = np.arange(128)[:, None]
    d = m - k
    bk = rel_bucket_np(np.clip(d, 0, None))
    tb = np.empty((128, 3, 640), np.float32)
    tb[:, 0] = np.where(d >= 0, np.float32(0), np.float32(NEG))
    tb[:, 1] = np.where(d >= 0, rel_bias[bk, j], np.float32(NEG))
    tb[:, 2] = np.where(d >= 0, rel_bias[bk, 4 + j], np.float32(NEG))
    return tb


WA_COLS = 1025
C_FOXQ, C_FOXK, C_DQ, C_DK, C_MQ, C_MK, C_CQ, C_CKV, C_KR, C_KRS, C_V3 = 0, 65, 129, 193, 257, 321, 385, 641, 769, 801, 833


class Stg:
    def __init__(self, kb, name, shape, dtype, n=2):
        self.t = [kb.sb(name, shape, dtype) for _ in range(n)]
        self.r = [Res(name) for _ in range(n)]
        self.d = [kb.dsem(name) for _ in range(n)]
        self.i = -1
        self.n = n

    def next(self):
        self.i = (self.i + 1) % self.n
        return self.t[self.i], self.r[self.i], self.d[self.i]


def stage_a1(kb, io, xT_f32, nchunks=NG, chunk0=0):
    nc = kb.nc
    pe, act, dve, pool, sp = kb.pe, kb.act, kb.dve, kb.pool, kb.sp
    xT = io["xT"]
    r_c = Res("a1c")
    dc = kb.dsem("a1c")
    wA = kb.sb("wA", [128, 8, WA_COLS], BF16)
    r_wA = Res("wA")
    for kc in range(8):
        kb.dma(pool, wA[:, kc, :], io["wA"][kc * 128:(kc + 1) * 128, :], writes=[r_wA], ds=dc)
    wuq32 = kb.sb("wuq32", [128, 2, 128], F32)
    wukv32 = kb.sb("wukv32", [128, 128], F32)
    gq = kb.sb("gq", [128, 2], F32)
    gkv = kb.sb("gkv", [128, 1], F32)
    wuq = kb.sb("wuq", [128, 2, 128], BF16)
    wukv = kb.sb("wukv", [128, 128], BF16)
    negb = kb.sb("negb", [65, 1], F32)
    identf = kb.sb("identf", [128, 128], F32)
    onesb = kb.sb("onesb", [128, 128], BF16)
    ones_row = kb.sb("ones_row", [65, 512], F32)
    kmeanT = kb.sb("kmeanT", [64, 32], F32)
    epsq = kb.sb("epsq", [128, 1], F32)
    kb.dma(sp, wuq32[:], io["wuq"].rearrange("(kc p) c -> p kc c", p=128), writes=[r_c], ds=dc)
    kb.dma(sp, wukv32[:], io["wukv"][:, :], writes=[r_c], ds=dc)
    kb.dma(sp, gq[:], io["gq"][:, :], writes=[r_c], ds=dc)
    kb.dma(sp, gkv[:], io["gkv"][:, :], writes=[r_c], ds=dc)
    kb.dma(sp, negb[:], io["bfg"][:, :], writes=[r_c], ds=dc)
    kb.dma(sp, identf[:], io["ident"][:, :], writes=[r_c], ds=dc)
    cst = kb.sb("cst", [35, SEQ], BF16)
    r_cst = Res("cst")
    dcst = kb.dsem("cst")
    kb.dma(sp, cst[0:32, :], io["onehot"][:, :], writes=[r_cst], ds=dcst)
    kb.dma(sp, cst[32:35, :], io["ones3"][:, :], writes=[r_cst], ds=dcst)
    kb.dma(sp, io["KA_fox"][64:67, :], cst[32:35, :], reads=[r_cst], ds=dc)
    kb.dma(sp, io["KA_moba"][64:96, :], cst[0:32, :], reads=[r_cst], ds=dc)
    r_w2 = Res("w2")
    for kc in range(2):
        kb.op(dve, lambda kc=kc: nc.vector.tensor_scalar(out=wuq[:, kc, :], in0=wuq32[:, kc, :], scalar1=gq[:, kc:kc + 1], scalar2=None, op0=ALU.mult),
              reads=[r_c], writes=[r_w2])
    kb.op(dve, lambda: nc.vector.tensor_scalar(out=wukv[:], in0=wukv32[:], scalar1=gkv[:, 0:1], scalar2=None, op0=ALU.mult), reads=[r_c], writes=[r_w2])
    kb.op(dve, lambda: nc.vector.tensor_scalar(out=negb[:], in0=negb[:], scalar1=-1.0, scalar2=None, op0=ALU.mult), reads=[r_c], writes=[r_w2])
    kb.op(pool, lambda: nc.gpsimd.memset(onesb[:], 1.0), writes=[r_w2])
    kb.op(pool, lambda: nc.gpsimd.memset(ones_row[:], 1.0), writes=[r_w2])
    kb.op(pool, lambda: nc.gpsimd.memset(kmeanT[:], 0.0), writes=[r_w2])
    kb.op(pool, lambda: nc.gpsimd.memset(epsq[:], RMS_EPS), writes=[r_w2])

    xTc = [kb.sb("xTc", [128, 8, 512], BF16) for _ in range(2)]
    r_x = R("xTc0", "xTc1")
    d_x = [kb.dsem("xTc") for _ in range(2)]
    if xT_f32:
        xTf = kb.sb("xTf", [128, 8, 512], F32)
        r_xf = R("xf0", "xf1")
        d_xf = [kb.dsem("xf") for _ in range(2)]
    cs = [kb.sb("cs", [32, 2, 512], F32) for _ in range(2)]
    r_cs = R("cs0", "cs1")
    d_cs = [kb.dsem("cs") for _ in range(2)]
    P = [kb.ps("P", [128, 512]) for _ in range(8)]
    r_P = R(*["P"] * 8)
    pc = {"i": -1}

    def nextp():
        pc["i"] = (pc["i"] + 1) % 8
        return pc["i"]

    s_q = Stg(kb, "s_q", [128, 512], BF16, 3)
    s_row = Stg(kb, "s_row", [65, 4, 512], BF16)
    ffe = kb.sb("ffe", [65, 512], F32)
    r_ffe = Res("ffe")
    ncb = Stg(kb, "ncb", [65, 512], F32)
    r1b = kb.sb("r1b", [65, 512], F32)
    r_r1 = Res("r1")
    mq32 = kb.sb("mq32", [64, 512], F32)
    r_mq = Res("mq32")
    gsb = kb.sb("gsb", [128, 32], F32)
    top8 = kb.sb("top8", [128, 8], F32)
    Mt = kb.sb("Mt", [128, 32], F32)
    r_g = Res("gate")
    s_M = Stg(kb, "s_M", [32, 512], BF16)
    cqT = kb.sb("cqT", [128, 2, 512], BF16)
    sq = kb.sb("sq", [128, 2, 512], BF16)
    r_cq, r_sq = R("cqT", "sq")
    rq = kb.sb("rq", [128, 512], F32)
    r_rq = Res("rq")
    t1 = kb.sb("t1", [32, 512], F32)
    t2 = kb.sb("t2", [32, 512], F32)
    r_t = Res("t12")
    ckvT = kb.sb("ckvT", [128, 512], BF16)
    sqkv = kb.sb("sqkv", [128, 512], BF16)
    r_ckv, r_sqkv = R("ckvT", "sqkv")
    rkv = kb.sb("rkv", [128, 512], F32)
    r_rkv = Res("rkv")
    rt = kb.sb("rt", [128, 1], F32)
    r_rt = Res("rt")
    s_v = Stg(kb, "s_v", [128, 4, 65], BF16)
    s_v3 = Stg(kb, "s_v3", [128, 3, 4, 65], BF16)
    for t in s_v.t:
        kb.op(pool, lambda t=t: nc.gpsimd.memset(t[:], 1.0), writes=[r_w2])
    for t in s_v3.t:
        kb.op(pool, lambda t=t: nc.gpsimd.memset(t[:], 1.0), writes=[r_w2])
    for r in s_v.r + s_v3.r:
        r.lw = r_w2.lw
    prev_nc = None

    for tc in range(chunk0, nchunks):
        xb = tc % 2
        sl = slice(tc * 512, (tc + 1) * 512)
        if xT_f32:
            for hf in range(2):
                kb.dma(sp, xTf[:, hf * 4:(hf + 1) * 4, :], xT[hf * 512:(hf + 1) * 512, sl].rearrange("(kc p) t -> p kc t", p=128),
                       writes=[r_xf[hf]], ds=d_xf[hf])
                kb.op(pool, lambda hf=hf: nc.gpsimd.tensor_copy(out=xTc[xb][:, hf * 4:(hf + 1) * 4, :], in_=xTf[:, hf * 4:(hf + 1) * 4, :]),
                      reads=[r_xf[hf]], writes=[r_x[xb]])
        else:
            kb.dma(sp, xTc[xb][:], xT[:, sl].rearrange("(kc p) t -> p kc t", p=128), writes=[r_x[xb]], ds=d_x[xb])
        kb.dma(sp, cs[xb][:, 0, :], io["CC"][:, sl], writes=[r_cs[xb]], ds=d_cs[xb])
        kb.dma(sp, cs[xb][:, 1, :], io["SS"][:, sl], writes=[r_cs[xb]], ds=d_cs[xb])

        def grp(c0, ncols):
            p = nextp()
            for kc in range(8):
                kb.op(pe, lambda kc=kc: nc.tensor.matmul(P[p][0:ncols, :], lhsT=wA[:, kc, c0:c0 + ncols], rhs=xTc[xb][:, kc, :],
                                                         start=(kc == 0), stop=(kc == 7)),
                      reads=[r_wA, r_x[xb]], writes=[r_P[p]])
            return p

        def out_scaled(p, rows, scale, dst, eng="act"):
            t, r, d = s_q.next()
            if scale is None:
                kb.op(dve, lambda: nc.vector.tensor_copy(out=t[0:rows, :], in_=P[p][0:rows, :]), reads=[r_P[p]], writes=[r])
            else:
                kb.op(act, lambda: nc.scalar.mul(out=t[0:rows, :], in_=P[p][0:rows, :], mul=scale), reads=[r_P[p]], writes=[r])
            kb.dma(sp, dst, t[0:rows, :], reads=[r], ds=d)

        p = grp(C_FOXQ, 65)
        out_scaled(p, 64, 0.125, io["QA_fox"][0:64, sl])
        kb.op(act, lambda: nc.scalar.activation(out=ffe[64:65, :], in_=P[p][64:65, :], func=AF.Exp, bias=negb[64:65, 0:1], scale=-1.0),
              reads=[r_P[p], r_w2], writes=[r_ffe])
        kb.op(act, lambda: nc.scalar.activation(out=ffe[64:65, :], in_=ffe[64:65, :], func=AF.Ln, bias=1.0, scale=1.0), reads=[r_ffe], writes=[r_ffe])
        nt, nr, nd = ncb.next()
        init = 0.0 if prev_nc is None else prev_nc[0][64:65, 511:512]
        kb.op(dve, lambda: nc.vector.tensor_tensor_scan(out=nt[64:65, :], data0=ones_row[64:65, :], data1=ffe[64:65, :], initial=init,
                                                        op0=ALU.mult, op1=ALU.add),
              reads=[r_ffe, r_w2] + ([prev_nc[1]] if prev_nc else []), writes=[nr])
        prev_nc = (nt, nr)
        kb.dma(sp, io["negc"][0:1, sl], nt[64:65, :], reads=[nr], ds=nd)
        rt_, rr_, rd_ = s_row.next()
        kb.op(dve, lambda: nc.vector.tensor_scalar(out=rt_[64:65, 0, :], in0=nt[64:65, :], scalar1=-1.0, scalar2=None, op0=ALU.mult), reads=[nr], writes=[rr_])
        kb.op(dve, lambda: nc.vector.scalar_tensor_tensor(out=r1b[64:65, :], in0=nt[64:65, :], scalar=-1.0, in1=rt_[64:65, 0, :], op0=ALU.mult, op1=ALU.subtract),
              reads=[nr, rr_], writes=[r_r1])
        kb.op(dve, lambda: nc.vector.tensor_copy(out=rt_[64:65, 1, :], in_=r1b[64:65, :]), reads=[r_r1], writes=[rr_])
        kb.op(dve, lambda: nc.vector.tensor_tensor(out=r1b[64:65, :], in0=r1b[64:65, :], in1=rt_[64:65, 1, :], op=ALU.subtract), reads=[r_r1, rr_], writes=[r_r1])
        kb.op(dve, lambda: nc.vector.tensor_copy(out=rt_[64:65, 2, :], in_=r1b[64:65, :]), reads=[r_r1], writes=[rr_])
        for k3 in range(3):
            kb.dma(sp, io["QA_fox"][64 + k3:65 + k3, sl], rt_[64:65, k3, :], reads=[rr_], ds=rd_)
        out_scaled(grp(C_FOXK, 64), 64, None, io["KA_fox"][0:64, sl])
        out_scaled(grp(C_DQ, 64), 64, 32 ** -0.5, io["QA_diff"][:, sl])
        out_scaled(grp(C_DK, 64), 64, None, io["KA_diff"][:, sl])
        p = grp(C_MQ, 64)
        out_scaled(p, 64, 0.125, io["QA_moba"][0:64, sl])
        kb.op(dve, lambda: nc.vector.tensor_copy(out=mq32[:], in_=P[p][0:64, :]), reads=[r_P[p]], writes=[r_mq])
        p = grp(C_MK, 64)
        out_scaled(p, 64, None, io["KA_moba"][0:64, sl])
        kb.op(dve, lambda: nc.vector.tensor_reduce(out=kmeanT[:, 2 * tc:2 * tc + 2], in_=P[p][0:64, :].rearrange("p (b t) -> p b t", t=256),
                                                   axis=AX.X, op=ALU.add),
              reads=[r_P[p]], writes=[r_g])
        mt_, mr_, md_ = s_M.next()
        for tt in range(4):
            own = (4 * tc + tt) // 2
            if own <= 3:
                kb.op(pool, lambda: nc.gpsimd.memset(Mt[:], NEG), writes=[r_g])
                kb.op(pool, lambda: nc.gpsimd.memset(Mt[:, 0:own + 1], 0.0), writes=[r_g])
            else:
                pg = nextp()
                kb.op(pe, lambda: nc.tensor.matmul(P[pg][:, 0:32], lhsT=mq32[:, tt * 128:(tt + 1) * 128], rhs=kmeanT[:, :], start=True, stop=True),
                      reads=[r_mq, r_g], writes=[r_P[pg]])
                kb.op(dve, lambda: nc.vector.memset(gsb[:], -1e30), writes=[r_g])
                kb.op(dve, lambda: nc.vector.tensor_copy(out=gsb[:, 0:own], in_=P[pg][:, 0:own]), reads=[r_P[pg]], writes=[r_g])
                kb.op(dve, lambda: nc.vector.max(out=top8[:], in_=gsb[:]), reads=[r_g], writes=[r_g])
                kb.op(dve, lambda: nc.vector.tensor_scalar(out=Mt[:], in0=gsb[:], scalar1=top8[:, 2:3], scalar2=None, op0=ALU.is_ge), reads=[r_g], writes=[r_g])
                kb.op(dve, lambda: nc.vector.tensor_scalar(out=Mt[:], in0=Mt[:], scalar1=1.0, scalar2=-NEG, op0=ALU.subtract, op1=ALU.mult), reads=[r_g], writes=[r_g])
                kb.op(dve, lambda: nc.vector.memset(Mt[:, own:own + 1], 0.0), writes=[r_g])
            pt = nextp()
            kb.op(pe, lambda: nc.tensor.transpose(P[pt][0:32, 0:128], Mt[:, :], identf[:]), reads=[r_g, r_c], writes=[r_P[pt]])
            kb.op(act, lambda: nc.scalar.copy(out=mt_[:, tt * 128:(tt + 1) * 128], in_=P[pt][0:32, 0:128]), reads=[r_P[pt]], writes=[mr_])
        kb.dma(sp, io["QA_moba"][64:96, sl], mt_[:], reads=[mr_], ds=md_)
        p0, p1 = grp(C_CQ, 128), grp(C_CQ + 128, 128)
        for kc, p in enumerate((p0, p1)):
            kb.op(act, lambda: nc.scalar.copy(out=cqT[:, kc, :], in_=P[p][:, :]), reads=[r_P[p]], writes=[r_cq])
            kb.op(act, lambda: nc.scalar.activation(out=sq[:, kc, :], in_=P[p][:, :], func=AF.Square), reads=[r_P[p]], writes=[r_sq])
        pss = nextp()
        for kc in range(2):
            kb.op(pe, lambda: nc.tensor.matmul(P[pss][:, :], lhsT=onesb[:], rhs=sq[:, kc, :], start=(kc == 0), stop=(kc == 1)),
                  reads=[r_sq, r_w2], writes=[r_P[pss]])
        kb.op(act, lambda: nc.scalar.activation(out=rq[:], in_=P[pss][:, :], func=AF.Sqrt, bias=epsq[:, 0:1], scale=1.0 / 256), reads=[r_P[pss], r_w2], writes=[r_rq])
        kb.op(dve, lambda: nc.vector.reciprocal(out=rq[:], in_=rq[:]), reads=[r_rq], writes=[r_rq])
        pn, pr, psw = nextp(), nextp(), nextp()
        for (pp, c0, n) in ((pn, 0, 64), (pr, 64, 32), (psw, 96, 32)):
            for kc in range(2):
                kb.op(pe, lambda: nc.tensor.matmul(P[pp][0:n, :], lhsT=wuq[:, kc, c0:c0 + n], rhs=cqT[:, kc, :], start=(kc == 0), stop=(kc == 1)),
                      reads=[r_cq, r_w2], writes=[r_P[pp]])
        t, r, d = s_q.next()
        kb.op(dve, lambda: nc.vector.scalar_tensor_tensor(out=t[0:64, :], in0=P[pn][0:64, :], scalar=96 ** -0.5, in1=rq[0:64, :], op0=ALU.mult, op1=ALU.mult),
              reads=[r_P[pn], r_rq], writes=[r])
        kb.dma(sp, io["QA_mla"][32:96, sl], t[0:64, :], reads=[r], ds=d)
        kb.op(dve, lambda: nc.vector.tensor_tensor(out=t1[:], in0=P[pr][0:32, :], in1=cs[xb][:, 0, :], op=ALU.mult), reads=[r_P[pr], r_cs[xb]], writes=[r_t])
        kb.op(dve, lambda: nc.vector.tensor_tensor(out=t2[:], in0=P[psw][0:32, :], in1=cs[xb][:, 1, :], op=ALU.mult), reads=[r_P[psw], r_cs[xb]], writes=[r_t])
        kb.op(dve, lambda: nc.vector.tensor_tensor(out=t1[:], in0=t1[:], in1=t2[:], op=ALU.add), reads=[r_t], writes=[r_t])
        t, r, d = s_q.next()
        kb.op(dve, lambda: nc.vector.scalar_tensor_tensor(out=t[0:32, :], in0=t1[:], scalar=96 ** -0.5, in1=rq[0:32, :], op0=ALU.mult, op1=ALU.mult),
              reads=[r_t, r_rq], writes=[r])
        kb.dma(sp, io["QA_mla"][0:32, sl], t[0:32, :], reads=[r], ds=d)
        p = grp(C_CKV, 128)
        kb.op(act, lambda: nc.scalar.copy(out=ckvT[:], in_=P[p][:, :]), reads=[r_P[p]], writes=[r_ckv])
        kb.op(act, lambda: nc.scalar.activation(out=sqkv[:], in_=P[p][:, :], func=AF.Square), reads=[r_P[p]], writes=[r_sqkv])
        pss = nextp()
        kb.op(pe, lambda: nc.tensor.matmul(P[pss][:, :], lhsT=onesb[:], rhs=sqkv[:], start=True, stop=True), reads=[r_sqkv, r_w2], writes=[r_P[pss]])
        kb.op(act, lambda: nc.scalar.activation(out=rkv[:], in_=P[pss][:, :], func=AF.Sqrt, bias=epsq[:, 0:1], scale=1.0 / 128), reads=[r_P[pss], r_w2], writes=[r_rkv])
        kb.op(dve, lambda: nc.vector.reciprocal(out=rkv[:], in_=rkv[:]), reads=[r_rkv], writes=[r_rkv])
        pk = nextp()
        kb.op(pe, lambda: nc.tensor.matmul(P[pk][0:64, :], lhsT=wukv[:, 0:64], rhs=ckvT[:], start=True, stop=True), reads=[r_ckv, r_w2], writes=[r_P[pk]])
        t, r, d = s_q.next()
        kb.op(dve, lambda: nc.vector.tensor_tensor(out=t[0:64, :], in0=P[pk][0:64, :], in1=rkv[0:64, :], op=ALU.mult), reads=[r_P[pk], r_rkv], writes=[r])
        kb.dma(sp, io["KA_mla"][32:96, sl], t[0:64, :], reads=[r], ds=d)
        for tt in range(4):
            i = 4 * tc + tt
            pv = nextp()
            kb.op(pe, lambda: nc.tensor.matmul(P[pv][:, 0:64], lhsT=ckvT[:, tt * 128:(tt + 1) * 128], rhs=wukv[:, 64:128], start=True, stop=True),
                  reads=[r_ckv, r_w2], writes=[r_P[pv]])
            kb.op(pe, lambda: nc.tensor.matmul(P[pv][:, 64:65], lhsT=sqkv[:, tt * 128:(tt + 1) * 128], rhs=onesb[:, 0:1], start=True, stop=True),
                  reads=[r_sqkv, r_w2], writes=[r_P[pv]])
            kb.op(act, lambda: nc.scalar.activation(out=rt[:], in_=P[pv][:, 64:65], func=AF.Sqrt, bias=epsq[:, 0:1], scale=1.0 / 128), reads=[r_P[pv], r_w2], writes=[r_rt])
            kb.op(dve, lambda: nc.vector.reciprocal(out=rt[:], in_=rt[:]), reads=[r_rt], writes=[r_rt])
            if tt == 0:
                t, r, d = s_v.next()
            kb.op(dve, lambda: nc.vector.tensor_scalar(out=t[:, tt, 0:64], in0=P[pv][:, 0:64], scalar1=rt[:, 0:1], scalar2=None, op0=ALU.mult),
                  reads=[r_P[pv], r_rt], writes=[r])
            if tt == 3:
                kb.dma(sp, io["V_mla"][:, 4 * tc:4 * tc + 4, :], t[:], reads=[r], ds=d)
        pa, pb = grp(C_KR, 32), grp(C_KRS, 32)
        kb.op(dve, lambda: nc.vector.tensor_tensor(out=t1[:], in0=P[pa][0:32, :], in1=cs[xb][:, 0, :], op=ALU.mult), reads=[r_P[pa], r_cs[xb]], writes=[r_t])
        kb.op(dve, lambda: nc.vector.tensor_tensor(out=t2[:], in0=P[pb][0:32, :], in1=cs[xb][:, 1, :], op=ALU.mult), reads=[r_P[pb], r_cs[xb]], writes=[r_t])
        t, r, d = s_q.next()
        kb.op(dve, lambda: nc.vector.tensor_tensor(out=t[0:32, :], in0=t1[:], in1=t2[:], op=ALU.add), reads=[r_t], writes=[r])
        kb.dma(sp, io["KA_mla"][0:32, sl], t[0:32, :], reads=[r], ds=d)
        for tt in range(4):
            i = 4 * tc + tt
            pv = nextp()
            for kc in range(8):
                kb.op(pe, lambda: nc.tensor.matmul(P[pv][:, 0:192], lhsT=xTc[xb][:, kc, tt * 128:(tt + 1) * 128], rhs=wA[:, kc, C_V3:C_V3 + 192],
                                                   start=(kc == 0), stop=(kc == 7)),
                      reads=[r_x[xb], r_wA], writes=[r_P[pv]])
            if tt == 0:
                t, r, d = s_v3.next()
            kb.op(act, lambda: nc.scalar.copy(out=t[:, :, tt, 0:64], in_=P[pv][:, 0:192].rearrange("p (h d) -> p h d", h=3)), reads=[r_P[pv]], writes=[r])
            if tt == 3:
                for h, nm in enumerate(("V_fox", "V_diff", "V_moba")):
                    kb.dma(sp, io[nm][:, 4 * tc:4 * tc + 4, :], t[:, h, :, :], reads=[r], ds=d)


A1_OUTS = [("QA_fox", [67, SEQ], BF16), ("KA_fox", [67, SEQ], BF16), ("V_fox", [128, NT, 65], BF16),
           ("QA_diff", [64, SEQ], BF16), ("KA_diff", [64, SEQ], BF16), ("V_diff", [128, NT, 65], BF16),
           ("QA_moba", [96, SEQ], BF16), ("KA_moba", [96, SEQ], BF16), ("V_moba", [128, NT, 65], BF16),
           ("QA_mla", [96, SEQ], BF16), ("KA_mla", [96, SEQ], BF16), ("V_mla", [128, NT, 65], BF16),
           ("negc", [1, SEQ], F32)]
A1_INS = [("wA", [1024, WA_COLS], F32), ("wuq", [256, 128], F32), ("wukv", [128, 128], F32), ("gq", [128, 2], F32), ("gkv", [128, 1], F32),
          ("bfg", [65, 1], F32), ("ident", [128, 128], F32), ("ones3", [3, SEQ], BF16), ("onehot", [32, SEQ], BF16),
          ("CC", [32, SEQ], F32), ("SS", [32, SEQ], F32)]


def build_a1(xT_f32=True, nchunks=NG, chunk0=0):
    nc = bass.Bass("TRN2", target_bir_lowering=False)
    io = {}
    io["xT"] = nc.dram_tensor("xT", [1024, SEQ], F32 if xT_f32 else BF16, kind="ExternalInput").ap()
    for name, shape, dt in A1_INS:
        io[name] = nc.dram_tensor(name, list(shape), dt, kind="ExternalInput").ap()
    for name, shape, dt in A1_OUTS:
        io[name] = nc.dram_tensor(name, list(shape), dt, kind="ExternalOutput").ap()
    kb = KB(nc)
    stage_a1(kb, io, xT_f32, nchunks, chunk0)
    kb.finish()
    return nc


def rope_tables():
    inv = (10000.0 ** (-np.arange(0, 16, dtype=np.float32) * 2.0 / 32)).astype(np.float32)
    ang = np.arange(SEQ, dtype=np.float32)[:, None] * inv[None, :]
    cos, sin = np.cos(ang).astype(np.float32).T, np.sin(ang).astype(np.float32).T
    return np.ascontiguousarray(np.concatenate([cos, cos], 0)), np.ascontiguousarray(np.concatenate([-sin, sin], 0))


def a1_maps(l, j, w_in, b_forget, mla_q_norm, mla_kv_norm, mla_w_uq, mla_w_ukv):
    W = w_in[l]
    GW = 256
    fox0, diff0, moba0, mla0 = 0, 3 * GW + 4, 6 * GW + 4, 9 * GW + 4
    hs = slice(64 * j, 64 * j + 64)
    cols = []
    cols.append(W[:, fox0:fox0 + GW][:, hs])
    cols.append(W[:, fox0 + 3 * GW + j:fox0 + 3 * GW + j + 1])
    cols.append(W[:, fox0 + GW:fox0 + 2 * GW][:, hs])
    cols.append(W[:, diff0:diff0 + GW][:, hs])
    cols.append(W[:, diff0 + GW:diff0 + 2 * GW][:, hs])
    cols.append(W[:, moba0:moba0 + GW][:, hs])
    cols.append(W[:, moba0 + GW:moba0 + 2 * GW][:, hs])
    cols.append(W[:, mla0:mla0 + 256])
    cols.append(W[:, mla0 + 256:mla0 + 384])
    kr = W[:, mla0 + 384:mla0 + 416]
    cols.append(kr)
    cols.append(np.concatenate([kr[:, 16:32], kr[:, 0:16]], 1))
    cols.append(W[:, fox0 + 2 * GW:fox0 + 3 * GW][:, hs])
    cols.append(W[:, diff0 + 2 * GW:diff0 + 3 * GW][:, hs])
    cols.append(W[:, moba0 + 2 * GW:moba0 + 3 * GW][:, hs])
    wA = np.ascontiguousarray(np.concatenate(cols, 1))
    assert wA.shape[1] == WA_COLS
    uq = mla_w_uq[l][:, 96 * j:96 * j + 96]
    rope = uq[:, 64:96]
    wuq = np.ascontiguousarray(np.concatenate([uq, rope[:, 16:32], rope[:, 0:16]], 1))
    ukv = np.ascontiguousarray(mla_w_ukv[l][:, 128 * j:128 * j + 128])
    bfg = np.zeros((65, 1), np.float32)
    bfg[64, 0] = b_forget[l, j]
    CC, SS = rope_tables()
    return {"wA": wA, "wuq": wuq, "wukv": ukv,
            "gq": np.ascontiguousarray(mla_q_norm[l].reshape(2, 128).T), "gkv": np.ascontiguousarray(mla_kv_norm[l].reshape(128, 1)),
            "bfg": bfg, "ident": np.eye(128, dtype=np.float32), "ones3": np.ones((3, SEQ), NPBF),
            "onehot": (np.arange(SEQ)[None, :] // 256 == np.arange(32)[:, None]).astype(NPBF), "CC": CC, "SS": SS}


_PROGS = {}


def _prog(name, fn):
    if name not in _PROGS:
        _PROGS[name] = fn()
    return _PROGS[name]


def kernel(x, w_in, b_forget, diff_lambda, diff_subln, mla_q_norm, mla_kv_norm, mla_w_uq, mla_w_ukv, rel_bias,
           w_o, ln1_g, ln1_b, w_up, conv_w, conv_b, w_down, ln2_g, ln2_b):
    f = lambda a: np.ascontiguousarray(np.asarray(a, dtype=np.float32))
    x, w_in, b_forget, diff_lambda, diff_subln = f(x), f(w_in), f(b_forget), f(diff_lambda), f(diff_subln)
    mla_q_norm, mla_kv_norm, mla_w_uq, mla_w_ukv, rel_bias = f(mla_q_norm), f(mla_kv_norm), f(mla_w_uq), f(mla_w_ukv), f(rel_bias)
    w_o, ln1_g, ln1_b, w_up, conv_w, conv_b, w_down, ln2_g, ln2_b = map(f, (w_o, ln1_g, ln1_b, w_up, conv_w, conv_b, w_down, ln2_g, ln2_b))
    cores = list(range(NCORES))
    nc_a1 = _prog("a1", lambda: build_a1(True))
    nc_a2 = _prog("a2", build_a2)
    nc_b = _prog("b", build_b)
    xcur = x.copy()
    identb = np.eye(128).astype(NPBF)
    for l in range(DEPTH):
        lam_init = 0.8 - 0.6 * math.exp(-0.3 * l)
        maps = []
        for c in cores:
            b, j = divmod(c, 4)
            m = a1_maps(l, j, w_in, b_forget, mla_q_norm, mla_kv_norm, mla_w_uq, mla_w_ukv)
            m["xT"] = np.ascontiguousarray(xcur[b].T)
            maps.append(m)
        r1 = run_bass_kernel_spmd(nc_a1, maps, core_ids=cores).results
        maps = []
        for c in cores:
            b, j = divmod(c, 4)
            m = {k: np.asarray(r1[c][k]) for k, _, _ in A1_OUTS if k != "negc"}
            kbv = np.zeros((128, 8, 64), np.float32)
            negc = np.ascontiguousarray(np.asarray(r1[c]["negc"]).reshape(64, 128).T)
            kbv[:, 0] = negc
            kbv[:, 1] = negc
            kbv[:, 2] = rel_bias[31, j]
            kbv[:, 4] = rel_bias[31, 4 + j]
            m["KB"] = kbv
            m["TB"] = band_tables(rel_bias, j)
            m["identb"] = identb
            m["dlam"] = np.ascontiguousarray(diff_lambda[l].reshape(1, 128))
            m["dsub"] = np.ascontiguousarray(diff_subln[l].reshape(64, 1))
            lamc = np.empty((64, 2), np.float32)
            lamc[:, 0] = -lam_init
            lamc[:, 1] = 1.0 - lam_init
            m["lamc"] = lamc
            maps.append(m)
        r2 = run_bass_kernel_spmd(nc_a2, maps, core_ids=cores).results
        wm = b_weight_maps(l, w_o, w_up, w_down, conv_w, conv_b, ln1_g, ln1_b, ln2_g, ln2_b)
        OTb = []
        for b in range(BATCH):
            ot = np.empty((4, 4, 64, SEQ), NPBF)
            for j in range(4):
                ot[:, j] = np.asarray(r2[4 * b + j]["OTc"]).reshape(4, 64, SEQ)
            OTb.append(ot.reshape(1024, SEQ))
        maps = []
        for c in cores:
            b, r = divmod(c, 4)
            m = dict(wm)
            t0 = TOK * r
            OT = np.zeros((1024, 128 + TOK), NPBF)
            xx = np.zeros((128 + TOK, 1024), np.float32)
            lo = max(t0 - 128, 0)
            OT[:, 128 + TOK - (t0 + TOK - lo):] = OTb[b][:, lo:t0 + TOK]
            xx[128 + TOK - (t0 + TOK - lo):] = xcur[b][lo:t0 + TOK]
            m["OT"], m["x"] = OT, xx
            m["flag"] = np.full((128, 1), 0.0 if r == 0 else 1.0, np.float32)
            maps.append(m)
        r3 = run_bass_kernel_spmd(nc_b, maps, core_ids=cores).results
        for c in cores:
            b, r = divmod(c, 4)
            xcur[b][TOK * r:TOK * (r + 1)] = np.asarray(r3[c]["xo"])
    return xcur
```
